# Optimizing a Trainium2 kernel written in Bass

```python
import math, functools
import jax, jax.numpy as jnp
from jax import lax
import numpy as np

D_MODEL = 1024
BATCH = 8
SEQ = 2048
DEPTH = 4

PLE_DIM = 256
N_EVEN = (DEPTH + 1) // 2
N_ODD = DEPTH // 2
A_WIDTH = D_MODEL // 2
A_GROUPS = 4
A_GROUP_DIM = A_WIDTH // A_GROUPS
A_CHUNK = 128
B_HEADS = 4
B_HEAD_DIM = (D_MODEL // 2) // B_HEADS
B_WIDTH = B_HEADS * B_HEAD_DIM
B_CONV = 4
B_CHUNK = 64
C_HEADS = 8
C_HEAD_DIM = D_MODEL // (2 * C_HEADS)
C_VALUE_DIM = 2 * C_HEAD_DIM
Q_BLOCK = 128
D_FF = 4 * D_MODEL
EVEN_IN = 2 * A_WIDTH + 4 * B_WIDTH + 2 * B_HEADS
NORM_EPS = 1e-6
RES_SCALE = 0.5

kernel_name = 'hybrid_gmlp_gdn_diffattn_trunk'


def rms_norm(x, gain):
    xf = x.astype(jnp.float32)
    y = xf * lax.rsqrt(jnp.mean(xf * xf, axis=-1, keepdims=True) + NORM_EPS)
    return (y * gain.astype(jnp.float32)).astype(x.dtype)


def layer_norm(x, gain):
    xf = x.astype(jnp.float32)
    xc = xf - jnp.mean(xf, axis=-1, keepdims=True)
    y = xc * lax.rsqrt(jnp.mean(xc * xc, axis=-1, keepdims=True) + NORM_EPS)
    return (y * gain.astype(jnp.float32)).astype(x.dtype)


def l2_normalize(x):
    xf = x.astype(jnp.float32)
    return xf * lax.rsqrt(jnp.sum(xf * xf, axis=-1, keepdims=True) + NORM_EPS)


def causal_depthwise_conv(x, w):
    k_len, ch = w.shape
    return lax.conv_general_dilated(
        x, w[:, None, :].astype(x.dtype), window_strides=(1,), padding=[(k_len - 1, 0)],
        dimension_numbers=('NWC', 'WIO', 'NWC'), feature_group_count=ch)


def gmlp_spatial_gate(uv, v_gain, w_s, b_s):
    bsz, seq, _ = uv.shape
    u, v = jnp.split(uv, 2, axis=-1)
    v = v.reshape(bsz, seq // A_CHUNK, A_CHUNK, A_GROUPS, A_GROUP_DIM)
    v = layer_norm(v, v_gain.reshape(A_GROUPS, A_GROUP_DIM))
    causal = jnp.tril(jnp.ones((A_CHUNK, A_CHUNK), dtype=bool))
    w = jnp.where(causal, w_s, 0.0)
    s = jnp.einsum('gts,bnsgc->bntgc', w, v) + b_s.T[None, None, :, :, None]
    return u * s.reshape(bsz, seq, A_WIDTH)


def chunk_gated_delta_rule(q, k, v, g, beta):
    bsz, seq, heads, dk = q.shape
    dv = v.shape[-1]
    c = B_CHUNK
    n = seq // c
    f32 = jnp.float32

    def chunks(t):
        t = t.astype(f32).reshape((bsz, n, c, heads) + t.shape[3:])
        return jnp.moveaxis(t, 3, 1)

    q, k, v, g, beta = chunks(q), chunks(k), chunks(v), chunks(g), chunks(beta)
    gc = jnp.cumsum(g, axis=-1)
    causal = jnp.tril(jnp.ones((c, c), dtype=bool))
    strict = jnp.tril(jnp.ones((c, c), dtype=bool), k=-1)
    diff = gc[..., :, None] - gc[..., None, :]
    decay = jnp.where(causal, jnp.exp(jnp.where(causal, diff, 0.0)), 0.0)
    k_beta = k * beta[..., None]
    v_beta = v * beta[..., None]
    a = jnp.where(strict, jnp.einsum('bhnid,bhnjd->bhnij', k_beta, k) * decay, 0.0)
    t_mat = a + jnp.eye(c, dtype=f32)
    solve = functools.partial(lax.linalg.triangular_solve, left_side=True, lower=True,
                              unit_diagonal=True)
    u = solve(t_mat, v_beta)
    w = solve(t_mat, k_beta * jnp.exp(gc)[..., None])
    qk = jnp.einsum('bhnid,bhnjd->bhnij', q, k) * decay
    q_dec = q * jnp.exp(gc)[..., None]
    k_dec = k * jnp.exp(gc[..., -1:] - gc)[..., None]
    g_last = jnp.exp(gc[..., -1])
    xs = tuple(jnp.moveaxis(t, 2, 0) for t in (q_dec, k_dec, u, w, qk, g_last))

    def step(state, inp):
        q_c, k_c, u_c, w_c, qk_c, gl_c = inp
        v_new = u_c - jnp.einsum('bhck,bhkv->bhcv', w_c, state)
        o_c = (jnp.einsum('bhck,bhkv->bhcv', q_c, state)
               + jnp.einsum('bhij,bhjv->bhiv', qk_c, v_new))
        state = state * gl_c[..., None, None] + jnp.einsum('bhck,bhcv->bhkv', k_c, v_new)
        return state, o_c

    state0 = jnp.zeros((bsz, heads, dk, dv), f32)
    _, o = lax.scan(step, state0, xs)
    return jnp.transpose(o, (1, 0, 3, 2, 4)).reshape(bsz, seq, heads, dv)


def gated_deltanet(qkv, z, beta_logit, a_logit, conv_w, a_log, dt_bias, out_gain):
    bsz, seq, _ = qkv.shape
    qkv = jax.nn.silu(causal_depthwise_conv(qkv, conv_w))
    q, k, v = jnp.split(qkv, 3, axis=-1)
    shp = (bsz, seq, B_HEADS, B_HEAD_DIM)
    q = l2_normalize(q.reshape(shp)) * (B_HEAD_DIM ** -0.5)
    k = l2_normalize(k.reshape(shp))
    v = v.reshape(shp)
    beta = jax.nn.sigmoid(beta_logit.astype(jnp.float32))
    g = -jnp.exp(a_log.astype(jnp.float32)) * jax.nn.softplus(
        a_logit.astype(jnp.float32) + dt_bias.astype(jnp.float32))
    o = chunk_gated_delta_rule(q, k, v, g, beta).astype(qkv.dtype)
    o = rms_norm(o, out_gain) * jax.nn.silu(z.reshape(shp))
    return o.reshape(bsz, seq, B_WIDTH)


def alibi_slopes(heads):
    return 2.0 ** (-8.0 * jnp.arange(1, heads + 1, dtype=jnp.float32) / heads)


def diff_attention(qkv, q_gain, k_gain, lam_params, sub_gain, lam_init):
    bsz, seq, _ = qkv.shape
    q, k, v = jnp.split(qkv, 3, axis=-1)
    q = rms_norm(q.reshape(bsz, seq, C_HEADS, 2, C_HEAD_DIM), q_gain) * (C_HEAD_DIM ** -0.5)
    k = rms_norm(k.reshape(bsz, seq, C_HEADS, 2, C_HEAD_DIM), k_gain)
    v = v.reshape(bsz, seq, C_HEADS, C_VALUE_DIM)
    lp = lam_params.astype(jnp.float32)
    lam = jnp.exp(jnp.sum(lp[0] * lp[1])) - jnp.exp(jnp.sum(lp[2] * lp[3])) + lam_init
    slopes = alibi_slopes(C_HEADS)[None, :, None, None, None]
    nblk = seq // Q_BLOCK
    qb = jnp.moveaxis(q.reshape(bsz, nblk, Q_BLOCK, C_HEADS, 2, C_HEAD_DIM), 1, 0)
    kpos = jnp.arange(seq)

    def attend(args):
        q_blk, blk = args
        qpos = blk * Q_BLOCK + jnp.arange(Q_BLOCK)
        dist = (qpos[:, None] - kpos[None, :]).astype(jnp.float32)
        logits = jnp.einsum('bqhrd,bkhrd->bhrqk', q_blk, k).astype(jnp.float32) - slopes * dist
        logits = jnp.where(dist >= 0, logits, -jnp.inf)
        probs = jax.nn.softmax(logits, axis=-1)
        attn = probs[:, :, 0] - lam * probs[:, :, 1]
        return jnp.einsum('bhqk,bkhe->bqhe', attn.astype(v.dtype), v)

    o = lax.map(attend, (qb, jnp.arange(nblk)))
    o = jnp.moveaxis(o, 0, 1).reshape(bsz, seq, C_HEADS, C_VALUE_DIM)
    o = rms_norm(o, sub_gain) * (1.0 - lam_init)
    return o.reshape(bsz, seq, C_HEADS * C_VALUE_DIM)


def setup_inputs(seed: int = 0) -> dict:
    key = jax.random.key(seed)
    ks = jax.random.split(key, 32)
    f32 = jnp.float32

    def normal(k, shape, scale):
        return jax.random.normal(k, shape, f32) * scale

    def gain(k, shape):
        return 1.0 + 0.1 * jax.random.normal(k, shape, f32)

    dt = jnp.exp(jax.random.uniform(ks[9], (N_EVEN, B_HEADS), f32,
                                    minval=math.log(1e-3), maxval=math.log(1e-1)))
    return {
        'x': normal(ks[0], (BATCH, SEQ, D_MODEL), 1.0),
        'p': normal(ks[1], (DEPTH, BATCH, SEQ, PLE_DIM), 1.0),
        'ln_mix_e': gain(ks[2], (N_EVEN, D_MODEL)),
        'w_in_e': normal(ks[3], (N_EVEN, D_MODEL, EVEN_IN), D_MODEL ** -0.5),
        'gmlp_v_gain': gain(ks[4], (N_EVEN, A_WIDTH)),
        'gmlp_ws': normal(ks[5], (N_EVEN, A_GROUPS, A_CHUNK, A_CHUNK), A_CHUNK ** -0.5),
        'gmlp_bs': gain(ks[6], (N_EVEN, A_GROUPS, A_CHUNK)),
        'gdn_conv': normal(ks[7], (N_EVEN, B_CONV, 3 * B_WIDTH), B_CONV ** -0.5),
        'gdn_a_log': jnp.log(jax.random.uniform(ks[8], (N_EVEN, B_HEADS), f32, minval=1.0, maxval=16.0)),
        'gdn_dt_bias': dt + jnp.log(-jnp.expm1(-dt)),
        'gdn_out_gain': gain(ks[10], (N_EVEN, B_HEAD_DIM)),
        'w_out_e': normal(ks[11], (N_EVEN, A_WIDTH + B_WIDTH, D_MODEL), (A_WIDTH + B_WIDTH) ** -0.5 * RES_SCALE),
        'ln_mix_o': gain(ks[12], (N_ODD, D_MODEL)),
        'w_qkv_o': normal(ks[13], (N_ODD, D_MODEL, 3 * C_HEADS * C_VALUE_DIM), D_MODEL ** -0.5),
        'attn_q_gain': gain(ks[14], (N_ODD, C_HEAD_DIM)),
        'attn_k_gain': gain(ks[15], (N_ODD, C_HEAD_DIM)),
        'diff_lambda': normal(ks[16], (N_ODD, 4, C_HEAD_DIM), 0.1),
        'attn_sub_gain': gain(ks[17], (N_ODD, C_VALUE_DIM)),
        'w_out_o': normal(ks[18], (N_ODD, C_HEADS * C_VALUE_DIM, D_MODEL), (C_HEADS * C_VALUE_DIM) ** -0.5 * RES_SCALE),
        'ln_mlp': gain(ks[19], (DEPTH, D_MODEL)),
        'w_mlp1': normal(ks[20], (DEPTH, D_MODEL, D_FF), D_MODEL ** -0.5),
        'w_mlp2': normal(ks[21], (DEPTH, D_FF, D_MODEL), D_FF ** -0.5 * RES_SCALE),
        'ln_ple': gain(ks[22], (DEPTH, D_MODEL)),
        'w_ple_gate': normal(ks[23], (DEPTH, D_MODEL, D_MODEL), D_MODEL ** -0.5),
        'w_ple_proj': normal(ks[24], (DEPTH, PLE_DIM, D_MODEL), PLE_DIM ** -0.5 * RES_SCALE),
    }


def reference(x, p, ln_mix_e, w_in_e, gmlp_v_gain, gmlp_ws, gmlp_bs, gdn_conv, gdn_a_log,
              gdn_dt_bias, gdn_out_gain, w_out_e, ln_mix_o, w_qkv_o, attn_q_gain, attn_k_gain,
              diff_lambda, attn_sub_gain, w_out_o, ln_mlp, w_mlp1, w_mlp2, ln_ple, w_ple_gate,
              w_ple_proj):
    split_at = [2 * A_WIDTH, 2 * A_WIDTH + 3 * B_WIDTH, 2 * A_WIDTH + 4 * B_WIDTH,
                2 * A_WIDTH + 4 * B_WIDTH + B_HEADS]
    h = x
    for layer in range(DEPTH):
        i = layer // 2
        if layer % 2 == 0:
            hn = rms_norm(h, ln_mix_e[i])
            proj = hn @ w_in_e[i]
            a_uv, b_qkv, b_z, b_beta, b_a = jnp.split(proj, split_at, axis=-1)
            a_out = gmlp_spatial_gate(jax.nn.gelu(a_uv), gmlp_v_gain[i], gmlp_ws[i], gmlp_bs[i])
            b_out = gated_deltanet(b_qkv, b_z, b_beta, b_a, gdn_conv[i], gdn_a_log[i],
                                   gdn_dt_bias[i], gdn_out_gain[i])
            h = h + jnp.concatenate([a_out, b_out], axis=-1) @ w_out_e[i]
        else:
            hn = rms_norm(h, ln_mix_o[i])
            lam_init = 0.8 - 0.6 * math.exp(-0.3 * layer)
            o = diff_attention(hn @ w_qkv_o[i], attn_q_gain[i], attn_k_gain[i], diff_lambda[i],
                               attn_sub_gain[i], lam_init)
            h = h + o @ w_out_o[i]
        hn = rms_norm(h, ln_mlp[layer])
        h = h + jnp.square(jax.nn.relu(hn @ w_mlp1[layer])) @ w_mlp2[layer]
        gate = jax.nn.sigmoid(rms_norm(h, ln_ple[layer]) @ w_ple_gate[layer])
        h = h + (p[layer] @ w_ple_proj[layer]) * gate
    return h
```

```python
import math
import numpy as np
import concourse.bass as bass
import concourse.mybir as mybir
from concourse.bass_utils import run_bass_kernel_spmd

F32 = mybir.dt.float32
BF16 = mybir.dt.bfloat16
AF = mybir.ActivationFunctionType
ALU = mybir.AluOpType
AX = mybir.AxisListType

ENGS = ["tensor", "vector", "scalar", "gpsimd", "sync"]
N_DMA_SEMS = 24
SAME_DIST = 6
EPS = 1e-6
SEQ = 2048
NTB = 4
DEPTH = 4


class Op:
    __slots__ = ("eng", "fn", "deps", "is_dma", "sem", "val", "clock", "idx")


class Sched:
    def __init__(self, nc):
        self.nc = nc
        self.ops = []
        self.last_w = {}
        self.readers = {}

    def op(self, eng, fn, reads=(), writes=(), dma=False):
        o = Op()
        o.eng = eng
        o.fn = fn
        o.is_dma = dma
        o.idx = len(self.ops)
        deps = set()
        for r in reads:
            w = self.last_w.get(r)
            if w is not None:
                deps.add(w)
        for r in writes:
            w = self.last_w.get(r)
            if w is not None:
                deps.add(w)
            for rd in self.readers.get(r, ()):
                deps.add(rd)
        deps.discard(o.idx)
        o.deps = sorted(deps)
        for r in reads:
            self.readers.setdefault(r, []).append(o.idx)
        for r in writes:
            self.last_w[r] = o.idx
            self.readers[r] = []
        self.ops.append(o)
        return o

    def emit(self):
        nc = self.nc
        ops = self.ops
        eng_sem = {e: nc.alloc_semaphore(name="es_" + e) for e in ENGS}
        dma_sems = [nc.alloc_semaphore(name="ds%d" % i) for i in range(N_DMA_SEMS)]
        dma_val = [0] * N_DMA_SEMS
        dma_last = [None] * N_DMA_SEMS
        eng_cnt = {e: 0 for e in ENGS}
        nd = 0
        for o in ops:
            if o.fn is None:
                o.sem = None
                o.val = 0
                continue
            if o.is_dma:
                j = nd % N_DMA_SEMS
                nd += 1
                if dma_last[j] is not None and dma_last[j] not in o.deps:
                    o.deps = sorted(set(o.deps) | {dma_last[j]})
                dma_val[j] += 16
                o.sem = ("d", j)
                o.val = dma_val[j]
                dma_last[j] = o.idx
            else:
                eng_cnt[o.eng] += 1
                o.sem = ("e", o.eng)
                o.val = eng_cnt[o.eng]

        def semh(s):
            return eng_sem[s[1]] if s[0] == "e" else dma_sems[s[1]]

        per_eng = {e: [] for e in ENGS}
        for o in ops:
            per_eng[o.eng].append(o)
        eng_clock = {e: {} for e in ENGS}
        waits = {}
        for o in ops:
            ck = eng_clock[o.eng]
            wd = {}
            for d in o.deps:
                dop = ops[d]
                if dop.sem is None:
                    if dop.eng == o.eng:
                        continue
                    for s, v in dop.clock.items():
                        if ck.get(s, 0) < v:
                            if wd.get(s, 0) < v:
                                wd[s] = v
                            ck[s] = v
                    continue
                if (not dop.is_dma) and dop.eng == o.eng:
                    if o.eng == "tensor" or o.is_dma or o.fn is None:
                        continue
                    if o.eng != "gpsimd" and (o.val - dop.val) > SAME_DIST:
                        continue
                if ck.get(dop.sem, 0) >= dop.val:
                    continue
                if wd.get(dop.sem, 0) < dop.val:
                    wd[dop.sem] = dop.val
                for s, v in dop.clock.items():
                    if ck.get(s, 0) < v:
                        ck[s] = v
            waits[o.idx] = list(wd.items())
            if o.fn is None:
                o.clock = dict(ck)
            elif o.is_dma:
                oc = dict(ck)
                oc[o.sem] = o.val
                o.clock = oc
            else:
                oc = dict(ck)
                oc[o.sem] = o.val
                o.clock = oc
        self.n_waits = sum(len(w) for w in waits.values())
        final_waits = {}
        for o in ops:
            if o.is_dma:
                final_waits[o.sem] = max(final_waits.get(o.sem, 0), o.val)

        with nc.Block() as block:
            def mk(ename, final=False):
                def body(eng):
                    for o in per_eng[ename]:
                        for s, v in waits[o.idx]:
                            eng.wait_ge(semh(s), v)
                        if o.fn is None:
                            continue
                        ins = o.fn(eng)
                        ins.then_inc(semh(o.sem), 16 if o.is_dma else 1)
                    if final:
                        for s, v in final_waits.items():
                            eng.wait_ge(semh(s), v)
                        for e in ENGS:
                            if e != ename and eng_cnt[e] > 0:
                                eng.wait_ge(eng_sem[e], eng_cnt[e])
                return body
            block.tensor(mk("tensor"))
            block.vector(mk("vector"))
            block.scalar(mk("scalar"))
            block.gpsimd(mk("gpsimd"))
            block.sync(mk("sync", final=True))


def make_consts():
    j = np.arange(128)[:, None]
    i = np.arange(128)[None, :]
    ident = np.eye(128, dtype=np.float32)
    ones = np.ones((128, 128), np.float32)
    bd64 = ((j // 64) == (i // 64)).astype(np.float32)
    tri_ji = (i >= j).astype(np.float32)
    cb = np.concatenate([ident, ones, bd64, tri_ji], axis=1)
    mneg_ji_incl = np.where(i >= j, 0.0, -30000.0).astype(np.float32)
    mneg_ij_strict = np.where(i < j, 0.0, -30000.0).astype(np.float32)
    sm_ji = (i > j).astype(np.float32)
    bd32 = ((j // 32) == (i // 32)).astype(np.float32)
    cf = np.concatenate([ident, mneg_ji_incl, mneg_ij_strict, sm_ji, bd32], axis=1)
    pos = np.arange(SEQ)
    a = (pos // 16).astype(np.float32)
    b = (pos % 16).astype(np.float32)
    one = np.ones(SEQ, np.float32)
    al = np.stack([a, b, one, one, -16.0 * one, -one, 16.0 * a, b], axis=0).astype(np.float32)
    return cb.astype(np.float32), cf, al


CB_ID, CB_ONES, CB_BD, CB_TRI = 0, 128, 256, 384
CF_ID, CF_MJI, CF_MIJ, CF_SM, CF_BD32 = 0, 128, 256, 384, 512

PC_LN = 0
PC_CONV = 96
PC_OG = 192
PC_QG = 194
PC_KG = 196
PC_SG = 198
PC_LI = 200
PC_N = 204
PR_VG = 0
PR_BS = 512
PR_AL = 1024
PR_DT = 1028
PR_LAM = 0
PR_N = 1032


def pack_params(inp, layers):
    layers = list(layers)
    ev = [l for l in layers if l % 2 == 0]
    od = [l for l in layers if l % 2 == 1]
    pc = np.zeros((128, PC_N), np.float32)
    for j, l in enumerate(layers):
        i = l // 2
        lnm = inp["ln_mix_e"][i] if l % 2 == 0 else inp["ln_mix_o"][i]
        for w, g in enumerate([lnm, inp["ln_mlp"][l], inp["ln_ple"][l]]):
            pc[:, PC_LN + (j * 3 + w) * 8: PC_LN + (j * 3 + w) * 8 + 8] = np.asarray(g).reshape(8, 128).T
    for e, l in enumerate(ev):
        i = l // 2
        cw = np.asarray(inp["gdn_conv"][i])
        pc[:, PC_CONV + e * 48: PC_CONV + (e + 1) * 48] = cw.reshape(4, 12, 128).transpose(2, 1, 0).reshape(128, 48)
        pc[:, PC_OG + e] = np.asarray(inp["gdn_out_gain"][i])
    for o, l in enumerate(od):
        i = l // 2
        pc[:, PC_QG + o] = np.tile(np.asarray(inp["attn_q_gain"][i]), 2)
        pc[:, PC_KG + o] = np.tile(np.asarray(inp["attn_k_gain"][i]), 2)
        pc[:, PC_SG + o] = np.asarray(inp["attn_sub_gain"][i])
        lam_init = 0.8 - 0.6 * math.exp(-0.3 * l)
        pc[:, PC_LI + 2 * o] = lam_init
        pc[:, PC_LI + 2 * o + 1] = 1.0 - lam_init
    pr = np.zeros((len(layers), 128, PR_N), np.float32)
    for j, l in enumerate(layers):
        i = l // 2
        if l % 2 == 0:
            row = np.concatenate([np.asarray(inp["gmlp_v_gain"][i]).reshape(-1), np.asarray(inp["gmlp_bs"][i]).reshape(-1),
                                  np.asarray(inp["gdn_a_log"][i]).reshape(-1), np.asarray(inp["gdn_dt_bias"][i]).reshape(-1)])
            pr[j] = np.broadcast_to(row[None, :], (128, PR_N))
        else:
            pr[j, :, 0:256] = np.broadcast_to(np.asarray(inp["diff_lambda"][i]).reshape(-1)[None, :], (128, 256))
    return pc, pr


WNAMES = ["w_in_e", "w_out_e", "w_qkv_o", "w_out_o", "w_mlp1", "w_mlp2", "w_ple_gate", "w_ple_proj"]
WSHAPES = {"w_in_e": [1024, 3080], "w_out_e": [1024, 1024], "w_qkv_o": [1024, 3072],
           "w_out_o": [1024, 1024], "w_mlp1": [1024, 4096], "w_mlp2": [4096, 1024],
           "w_ple_gate": [1024, 1024], "w_ple_proj": [256, 1024]}
WKIND = {"w_in_e": "e", "w_out_e": "e", "w_qkv_o": "o", "w_out_o": "o", "w_mlp1": "a", "w_mlp2": "a",
         "w_ple_gate": "a", "w_ple_proj": "a"}

NP = 22
NWS = 3


class KB:
    def __init__(self, layers, taps=False):
        self.layers = list(layers)
        self.taps = taps
        nc = self.nc = bass.Bass("TRN2", target_bir_lowering=False)
        self.S = Sched(nc)

        def din(n, s):
            return nc.dram_tensor(n, list(s), F32, kind="ExternalInput").ap()
        L = self.layers
        ev = [l for l in L if l % 2 == 0]
        od = [l for l in L if l % 2 == 1]
        self.pos = {l: j for j, l in enumerate(L)}
        self.epos = {l: j for j, l in enumerate(ev)}
        self.opos = {l: j for j, l in enumerate(od)}
        cnt = {"e": len(ev), "o": len(od), "a": len(L)}
        self.d_x = din("xT", [1024, SEQ])
        self.d_p = din("pT", [len(L), 256, SEQ])
        self.d_w = {n: din(n, [cnt[WKIND[n]]] + WSHAPES[n]) for n in WNAMES if cnt[WKIND[n]] > 0}
        if ev:
            self.d_wst = din("gmlp_wsT", [len(ev), 4, 128, 128])
        self.d_pc = din("pcol", [128, PC_N])
        self.d_pr = din("prow", [len(L), 128, PR_N])
        self.d_cb = din("cb", [128, 512])
        self.d_cf = din("cf", [128, 640])
        if od:
            self.d_al = din("alibi", [8, SEQ])
        self.d_out = nc.dram_tensor("outT", [1024, SEQ], F32, kind="ExternalOutput").ap()
        if taps:
            self.d_tap = nc.dram_tensor("tap_mo", [1024, SEQ], F32, kind="ExternalOutput").ap()

        def sb(n, s, d):
            return nc.alloc_sbuf_tensor(n, list(s), d).ap()
        self.hT = sb("hT", [128, 8, SEQ], F32)
        self.hn = sb("hn", [128, 8, SEQ], BF16)
        self.mo = sb("mo", [128, 8, SEQ], BF16)
        self.ws = sb("ws", [128, NWS, 4096], BF16)
        self.P = sb("P", [128, NP, 512], F32)
        self.cb = sb("cbs", [128, 512], BF16)
        self.cf = sb("cfs", [128, 640], F32)
        self.pc = sb("pcs", [128, PC_N], F32)
        self.pr = sb("prs", [128, PR_N], F32)
        self.sm = sb("small", [128, 64], F32)
        self.xp = sb("xp", [128, 516], F32)
        self.hal = sb("hal", [128, 3, 4], F32)
        self.Sst = sb("Sst", [128, 128], F32)
        self.w8 = sb("w8", [128, 8, 8], BF16)
        self.st = sb("bnst", [128, 4, 6], F32)
        self.mv = sb("bnmv", [128, 4, 2], F32)
        self.ps = nc.alloc_psum_tensor("ps", [128, 8, 512], F32).ap()
        self.bank_rr = 0
        self.ws_rr = 0
        self.build()
        self.S.emit()

    def bank(self):
        b = self.bank_rr % 4
        self.bank_rr += 1
        return b

    def wslot(self):
        s = self.ws_rr % NWS
        self.ws_rr += 1
        return s

    def F(self, i):
        return self.P[:, i, :]

    def H(self, i, h):
        return self.P[:, i, :].bitcast(BF16)[:, h * 512:(h + 1) * 512]

    def W(self, i0, n):
        return self.P[:, i0:i0 + n, :].bitcast(BF16).rearrange("p a b -> p (a b)")

    @staticmethod
    def fk(i):
        return [("P", i, 0), ("P", i, 1)]

    @staticmethod
    def hk(i, h):
        return [("P", i, h)]

    @staticmethod
    def wk(i0, tb):
        return [("P", i0 + tb // 2, tb % 2)]

    def mm(self, out, lhsT, rhs, start, stop, reads, writes):
        self.S.op("tensor", lambda e: e.matmul(out, lhsT=lhsT, rhs=rhs, start=start, stop=stop), reads, writes)

    def tr(self, out, in_, reads, writes):
        ident = self.cf[:, CF_ID:CF_ID + 128]
        self.S.op("tensor", lambda e: e.transpose(out, in_, ident), list(reads) + ["cf"], writes)

    def act(self, out, in_, func, reads, writes, bias=None, scale=None):
        kw = {}
        if bias is not None:
            kw["bias"] = bias
        if scale is not None:
            kw["scale"] = scale
        self.S.op("scalar", lambda e: e.activation(out=out, in_=in_, func=func, **kw), reads, writes)

    def tt(self, out, in0, in1, op, reads, writes, eng="vector"):
        self.S.op(eng, lambda e: e.tensor_tensor(out=out, in0=in0, in1=in1, op=op), reads, writes)

    def ts(self, out, in0, s1, op0, reads, writes, s2=None, op1=None, eng="vector"):
        if op1 is None:
            self.S.op(eng, lambda e: e.tensor_scalar(out=out, in0=in0, scalar1=s1, scalar2=None, op0=op0), reads, writes)
        else:
            self.S.op(eng, lambda e: e.tensor_scalar(out=out, in0=in0, scalar1=s1, scalar2=s2, op0=op0, op1=op1), reads, writes)

    def stt(self, out, in0, scalar, in1, op0, op1, reads, writes):
        self.S.op("vector", lambda e: e.scalar_tensor_tensor(out=out, in0=in0, scalar=scalar, in1=in1, op0=op0, op1=op1), reads, writes)

    def recip(self, out, in_, reads, writes):
        self.S.op("vector", lambda e: e.reciprocal(out=out, in_=in_), reads, writes)

    def cp(self, out, in_, reads, writes, eng="vector"):
        self.S.op(eng, lambda e: e.tensor_copy(out=out, in_=in_), reads, writes)

    def mset(self, ap, val, writes, eng="vector"):
        self.S.op(eng, lambda e: e.memset(ap, val), [], writes)

    def gelu(self, out, psap, tmp, reads, writes):
        T = self.F(tmp)
        tk = self.fk(tmp)
        self.act(T, psap, AF.Square, reads, tk)
        self.ts(T, T, 0.044715, ALU.mult, tk, tk, s2=1.0, op1=ALU.add)
        self.tt(T, T, psap, ALU.mult, tk + list(reads), tk)
        self.act(T, T, AF.Sigmoid, tk, tk, scale=2.0 * math.sqrt(2.0 / math.pi))
        self.tt(out, T, psap, ALU.mult, tk + list(reads), writes)

    def dma(self, out, in_, reads, writes, eng="sync"):
        self.S.op(eng, lambda e: e.dma_start(out=out, in_=in_), reads, writes, dma=True)

    @staticmethod
    def wkeys(slot):
        return [("ws", slot, j) for j in range(4)]

    def acquire(self, slot):
        self.S.op("gpsimd", None, [], self.wkeys(slot))

    def wview(self, slot, a, b, off=0):
        return self.ws[:, slot, off:off + a * b].rearrange("p (a b) -> p a b", a=a)

    def build(self):
        self.dma(self.cb, self.d_cb, [], ["cb"], eng="gpsimd")
        self.dma(self.cf, self.d_cf, [], ["cf"])
        self.dma(self.pc, self.d_pc, [], ["pc"])
        xv = self.d_x.rearrange("(c p) n -> p c n", p=128)
        for c in range(8):
            self.dma(self.hT[:, c, :], xv[:, c, :], [], [("h", c, tb) for tb in range(NTB)])
        for l in self.layers:
            if l % 2 == 0:
                self.even_layer(l)
            else:
                self.attn_layer(l)
            self.mlp(l)
            self.ple(l)
        ov = self.d_out.rearrange("(c p) n -> p c n", p=128)
        for c in range(8):
            self.dma(ov[:, c, :], self.hT[:, c, :], [("h", c, tb) for tb in range(NTB)], [])

    def rmsnorm(self, l, which):
        col0 = PC_LN + (self.pos[l] * 3 + which) * 8
        ones = self.cb[:, CB_ONES:CB_ONES + 128]
        for tb in range(NTB):
            tsl = slice(tb * 512, (tb + 1) * 512)
            b = self.bank()
            for c in range(8):
                hh = c % 2
                self.act(self.H(19, hh), self.hT[:, c, tsl], AF.Square, [("h", c, tb)], self.hk(19, hh))
                self.mm(self.ps[:, b, :], ones, self.H(19, hh), c == 0, c == 7, self.hk(19, hh) + ["cb"], [("ps", b)])
            f = 17 + (tb % 2)
            self.act(self.F(f), self.ps[:, b, :], AF.Sqrt, [("ps", b)], self.fk(f), bias=EPS, scale=1.0 / 1024)
            self.recip(self.F(f), self.F(f), self.fk(f), self.fk(f))
            for c in range(8):
                self.stt(self.hn[:, c, tsl], self.hT[:, c, tsl], self.pc[:, col0 + c:col0 + c + 1], self.F(f),
                         ALU.mult, ALU.mult, [("h", c, tb), "pc"] + self.fk(f), [("hn", c, tb)])

    def mlp(self, l):
        self.rmsnorm(l, 1)
        w1 = self.d_w["w_mlp1"][self.pos[l]].rearrange("(c p) n -> p c n", p=128)
        w2 = self.d_w["w_mlp2"][self.pos[l]].rearrange("(c p) n -> p c n", p=128)
        a1 = [(13, 0), (13, 1), (14, 0), (14, 1)]
        for g in range(8):
            s1 = self.wslot()
            v1 = self.wview(s1, 8, 512)
            self.dma(v1, w1[:, :, g * 512:(g + 1) * 512], [], self.wkeys(s1), eng="gpsimd")
            s2 = self.wslot()
            v2 = self.wview(s2, 4, 1024)
            self.dma(v2, w2[:, g * 4:(g + 1) * 4, :], [], self.wkeys(s2), eng="gpsimd")
            for tb in range(NTB):
                tsl = slice(tb * 512, (tb + 1) * 512)
                for f in range(4):
                    b = self.bank()
                    for kc in range(8):
                        self.mm(self.ps[:, b, :], v1[:, kc, f * 128:(f + 1) * 128], self.hn[:, kc, tsl], kc == 0, kc == 7,
                                self.wkeys(s1) + [("hn", kc, tb)], [("ps", b)])
                    fr = 15 + (f % 2)
                    self.act(self.F(fr), self.ps[:, b, :], AF.Relu, [("ps", b)], self.fk(fr))
                    self.act(self.H(*a1[f]), self.F(fr), AF.Square, self.fk(fr), self.hk(*a1[f]))
                for dc in range(8):
                    b = self.bank()
                    for f in range(4):
                        self.mm(self.ps[:, b, :], v2[:, f, dc * 128:(dc + 1) * 128], self.H(*a1[f]), f == 0, f == 3,
                                self.wkeys(s2) + self.hk(*a1[f]), [("ps", b)])
                    self.tt(self.hT[:, dc, tsl], self.hT[:, dc, tsl], self.ps[:, b, :], ALU.add,
                            [("h", dc, tb), ("ps", b)], [("h", dc, tb)])

    def ple(self, l):
        self.rmsnorm(l, 2)
        wg = self.d_w["w_ple_gate"][self.pos[l]].rearrange("(c p) n -> p c n", p=128)
        wp = self.d_w["w_ple_proj"][self.pos[l]].rearrange("(c p) n -> p c n", p=128)
        sp = self.wslot()
        vp = self.wview(sp, 2, 1024)
        vpt = self.wview(sp, 2, 512, off=2048)
        self.acquire(sp)
        self.dma(vp, wp, [], [("ws", sp, 0)], eng="gpsimd")
        pv = self.d_p[self.pos[l]].rearrange("(c p) n -> p c n", p=128)
        for half in range(2):
            sg = self.wslot()
            if sg == sp:
                sg = self.wslot()
            vg = self.wview(sg, 8, 512)
            self.dma(vg, wg[:, :, half * 512:(half + 1) * 512], [], self.wkeys(sg), eng="gpsimd")
            for tb in range(NTB):
                tsl = slice(tb * 512, (tb + 1) * 512)
                self.dma(vpt, pv[:, :, tsl], [], [("ws", sp, 1)], eng="gpsimd")
                for d4 in range(4):
                    dc = half * 4 + d4
                    b = self.bank()
                    for kc in range(8):
                        self.mm(self.ps[:, b, :], vg[:, kc, d4 * 128:(d4 + 1) * 128], self.hn[:, kc, tsl], kc == 0, kc == 7,
                                self.wkeys(sg) + [("hn", kc, tb)], [("ps", b)])
                    fg = 15 + (d4 % 2)
                    self.act(self.F(fg), self.ps[:, b, :], AF.Sigmoid, [("ps", b)], self.fk(fg))
                    b2 = self.bank()
                    for kc in range(2):
                        self.mm(self.ps[:, b2, :], vp[:, kc, dc * 128:(dc + 1) * 128], vpt[:, kc, :], kc == 0, kc == 1,
                                [("ws", sp, 0), ("ws", sp, 1)], [("ps", b2)])
                    self.tt(self.F(fg), self.F(fg), self.ps[:, b2, :], ALU.mult, self.fk(fg) + [("ps", b2)], self.fk(fg))
                    self.tt(self.hT[:, dc, tsl], self.hT[:, dc, tsl], self.F(fg), ALU.add,
                            [("h", dc, tb)] + self.fk(fg), [("h", dc, tb)])

    def out_proj(self, wv):
        if self.taps:
            tv = self.d_tap.rearrange("(c p) n -> p c n", p=128)
            for c in range(8):
                self.dma(tv[:, c, :], self.mo[:, c, :], [("mo", c, tb) for tb in range(NTB)], [], eng="gpsimd")
        for half in range(2):
            so = self.wslot()
            vo = self.wview(so, 8, 512)
            self.dma(vo, wv[:, :, half * 512:(half + 1) * 512], [], self.wkeys(so), eng="gpsimd")
            for tb in range(NTB):
                tsl = slice(tb * 512, (tb + 1) * 512)
                for d4 in range(4):
                    dc = half * 4 + d4
                    b = self.bank()
                    for kc in range(8):
                        self.mm(self.ps[:, b, :], vo[:, kc, d4 * 128:(d4 + 1) * 128], self.mo[:, kc, tsl], kc == 0, kc == 7,
                                self.wkeys(so) + [("mo", kc, tb)], [("ps", b)])
                    self.tt(self.hT[:, dc, tsl], self.hT[:, dc, tsl], self.ps[:, b, :], ALU.add,
                            [("h", dc, tb), ("ps", b)], [("h", dc, tb)])

    def attn_layer(self, l):
        i = self.opos[l]
        S = self.S
        self.dma(self.pr, self.d_pr[self.pos[l]], [], ["pr"])
        sm = self.sm
        lp = self.pr[:, PR_LAM:PR_LAM + 256]
        self.tt(self.F(13)[:, 0:64], lp[:, 0:64], lp[:, 64:128], ALU.mult, ["pr"], self.fk(13))
        self.tt(self.F(13)[:, 64:128], lp[:, 128:192], lp[:, 192:256], ALU.mult, ["pr"], self.fk(13))
        S.op("vector", lambda e: e.reduce_sum(out=sm[:, 0:2], in_=self.F(13)[:, 0:128].rearrange("p (a b) -> p a b", a=2), axis=AX.X),
             self.fk(13), ["sm"])
        self.act(sm[:, 2:4], sm[:, 0:2], AF.Exp, ["sm"], ["sm"])
        self.tt(sm[:, 4:5], sm[:, 3:4], sm[:, 2:3], ALU.subtract, ["sm"], ["sm"])
        self.tt(sm[:, 5:6], sm[:, 4:5], self.pc[:, PC_LI + 2 * i:PC_LI + 2 * i + 1], ALU.subtract, ["sm", "pc"], ["sm"])
        self.ts(sm[:, 6:7], self.pc[:, PC_QG + i:PC_QG + i + 1], 0.125, ALU.mult, ["pc"], ["sm"])
        self.tt(sm[:, 7:8], self.pc[:, PC_SG + i:PC_SG + i + 1], self.pc[:, PC_LI + 2 * i + 1:PC_LI + 2 * i + 2], ALU.mult, ["pc"], ["sm"])
        nlam = sm[:, 5:6]
        gq = sm[:, 6:7]
        gk = self.pc[:, PC_KG + i:PC_KG + i + 1]
        gs = sm[:, 7:8]

        self.rmsnorm(l, 0)
        qA = [self.W(0, 2), self.W(2, 2)]
        kA = [self.W(4, 2), self.W(6, 2)]
        QI = [0, 2]
        KI = [4, 6]
        vT = self.W(8, 2)
        E = [(10, 0), (10, 1), (11, 0), (11, 1)]
        ones = self.cb[:, CB_ONES:CB_ONES + 128]
        bd = self.cb[:, CB_BD:CB_BD + 128]
        tri = self.cb[:, CB_TRI:CB_TRI + 128]
        wq = self.d_w["w_qkv_o"][i].rearrange("(c p) n -> p c n", p=128)
        allq = lambda r: [k for tb in range(NTB) for k in self.wk(QI[r], tb)]
        allk = lambda r: [k for tb in range(NTB) for k in self.wk(KI[r], tb)]
        for r in range(2):
            self.dma(qA[r][64:68, :], self.d_al[0:4, :], [], allq(r), eng="gpsimd")
        self.dma(qA[0][96:100, :], self.d_al[4:8, :], [], allq(0), eng="gpsimd")
        for h in range(8):
            slope = 2.0 ** (-(h + 1))
            sw = self.wslot()
            vw = self.wview(sw, 8, 384)
            self.acquire(sw)
            for j in range(3):
                self.dma(vw[:, :, j * 128:(j + 1) * 128], wq[:, :, j * 1024 + h * 128: j * 1024 + (h + 1) * 128], [],
                         [("ws", sw, j)], eng="gpsimd")
            for r in range(2):
                self.ts(kA[r][64:68, :], qA[0][96:100, :], slope, ALU.mult, allq(0), allk(r))
            for tb in range(NTB):
                tsl = slice(tb * 512, (tb + 1) * 512)
                for which in range(2):
                    b = self.bank()
                    for kc in range(8):
                        self.mm(self.ps[:, b, :], vw[:, kc, which * 128:(which + 1) * 128], self.hn[:, kc, tsl], kc == 0, kc == 7,
                                [("ws", sw, which), ("hn", kc, tb)], [("ps", b)])
                    self.act(self.H(12, 0), self.ps[:, b, :], AF.Square, [("ps", b)], self.hk(12, 0))
                    b2 = self.bank()
                    self.mm(self.ps[:, b2, :], bd, self.H(12, 0), True, True, self.hk(12, 0) + ["cb"], [("ps", b2)])
                    self.act(self.F(16), self.ps[:, b2, :], AF.Sqrt, [("ps", b2)], self.fk(16), bias=EPS, scale=1.0 / 64)
                    self.recip(self.F(16), self.F(16), self.fk(16), self.fk(16))
                    dst = qA if which == 0 else kA
                    gcol = gq if which == 0 else gk
                    DI = QI if which == 0 else KI
                    for r in range(2):
                        self.stt(dst[r][0:64, tsl], self.ps[r * 64:(r + 1) * 64, b, :], gcol[r * 64:(r + 1) * 64, :],
                                 self.F(16)[r * 64:(r + 1) * 64, :], ALU.mult, ALU.mult,
                                 [("ps", b), "sm", "pc"] + self.fk(16), self.wk(DI[r], tb))
                b = self.bank()
                for t in range(4):
                    tok = slice(tb * 512 + t * 128, tb * 512 + (t + 1) * 128)
                    for kc in range(8):
                        self.mm(self.ps[:, b, t * 128:(t + 1) * 128], self.hn[:, kc, tok], vw[:, kc, 256:384], kc == 0, kc == 7,
                                [("ws", sw, 2), ("hn", kc, tb)], [("ps", b)])
                self.act(vT[:, tsl], self.ps[:, b, :], AF.Copy, [("ps", b)], self.wk(8, tb))
            for qb in range(NTB):
                qsl = slice(qb * 512, (qb + 1) * 512)
                nkt = 4 * qb + 4
                for r in range(2):
                    bO, bZ = 4 + 2 * r, 5 + 2 * r
                    for kt in range(nkt):
                        d = kt - 4 * qb
                        q0 = 128 * d if d > 0 else 0
                        b = self.bank()
                        self.mm(self.ps[:, b, q0:512], kA[r][0:68, kt * 128:(kt + 1) * 128], qA[r][0:68, qb * 512 + q0:(qb + 1) * 512],
                                True, True, self.wk(KI[r], kt // 4) + self.wk(QI[r], qb), [("ps", b)])
                        e = E[kt % 4]
                        self.act(self.H(*e)[:, q0:512], self.ps[:, b, q0:512], AF.Exp, [("ps", b)], self.hk(*e))
                        if d >= 0:
                            self.tt(self.H(*e)[:, q0:q0 + 128], self.H(*e)[:, q0:q0 + 128], tri, ALU.mult,
                                    self.hk(*e) + ["cb"], self.hk(*e))
                        self.mm(self.ps[:, bO, q0:512], vT[:, kt * 128:(kt + 1) * 128], self.H(*e)[:, q0:512], kt == 0, kt == nkt - 1,
                                self.wk(8, kt // 4) + self.hk(*e), [("ps", bO)])
                        self.mm(self.ps[:, bZ, q0:512], ones, self.H(*e)[:, q0:512], kt == 0, kt == nkt - 1,
                                self.hk(*e) + ["cb"], [("ps", bZ)])
                    self.recip(self.F(13 + r), self.ps[:, bZ, :], [("ps", bZ)], self.fk(13 + r))
                    self.tt(self.F(13 + r), self.F(13 + r), self.ps[:, bO, :], ALU.mult, self.fk(13 + r) + [("ps", bO)], self.fk(13 + r))
                self.stt(self.F(15), self.F(14), nlam, self.F(13), ALU.mult, ALU.add, self.fk(13) + self.fk(14) + ["sm"], self.fk(15))
                self.act(self.H(12, 1), self.F(15), AF.Square, self.fk(15), self.hk(12, 1))
                b = self.bank()
                self.mm(self.ps[:, b, :], ones, self.H(12, 1), True, True, self.hk(12, 1) + ["cb"], [("ps", b)])
                self.act(self.F(16), self.ps[:, b, :], AF.Sqrt, [("ps", b)], self.fk(16), bias=EPS, scale=1.0 / 128)
                self.recip(self.F(16), self.F(16), self.fk(16), self.fk(16))
                self.stt(self.mo[:, h, qsl], self.F(15), gs, self.F(16), ALU.mult, ALU.mult,
                         self.fk(15) + self.fk(16) + ["sm"], [("mo", h, qb)])
        self.out_proj(self.d_w["w_out_o"][i].rearrange("(c p) n -> p c n", p=128))

    def even_layer(self, l):
        i = self.epos[l]
        sm = self.sm
        self.dma(self.pr, self.d_pr[self.pos[l]], [], ["pr"])
        self.act(sm[:, 8:12], self.pr[:, PR_AL:PR_AL + 4], AF.Exp, ["pr"], ["sm"])
        self.ts(sm[:, 8:12], sm[:, 8:12], -1.0, ALU.mult, ["sm"], ["sm"])
        self.rmsnorm(l, 0)
        win = self.d_w["w_in_e"][i].rearrange("(c p) n -> p c n", p=128)
        self.dma(self.w8, win[:, :, 3072:3080], [], ["w8"], eng="gpsimd")
        ones = self.cb[:, CB_ONES:CB_ONES + 128]
        tri = self.cb[:, CB_TRI:CB_TRI + 128]
        su = self.wslot()
        vu = self.wview(su, 8, 512)
        self.dma(vu, win[:, :, 0:512], [], self.wkeys(su), eng="gpsimd")
        for c in range(4):
            for tb in range(NTB):
                tsl = slice(tb * 512, (tb + 1) * 512)
                b = self.bank()
                for kc in range(8):
                    self.mm(self.ps[:, b, :], vu[:, kc, c * 128:(c + 1) * 128], self.hn[:, kc, tsl], kc == 0, kc == 7,
                            self.wkeys(su) + [("hn", kc, tb)], [("ps", b)])
                self.gelu(self.mo[:, c, tsl], self.ps[:, b, :], (c * 4 + tb) % 2, [("ps", b)], [("mo", c, tb)])
        sv = self.wslot()
        vv = self.wview(sv, 8, 512)
        self.dma(vv, win[:, :, 512:1024], [], self.wkeys(sv), eng="gpsimd")
        self.dma(self.F(8).rearrange("p (g t) -> p g t", g=4), self.d_wst[i].rearrange("g s t -> s g t"), [], self.fk(8))
        wst = self.H(7, 0).rearrange("p (g t) -> p g t", g=4)
        self.tt(wst, self.F(8).rearrange("p (g t) -> p g t", g=4),
                tri.rearrange("p (o t) -> p o t", o=1).broadcast_to([128, 4, 128]), ALU.mult,
                self.fk(8) + ["cb"], self.hk(7, 0))
        st, mv = self.st, self.mv
        for n in range(16):
            tok = slice(n * 128, (n + 1) * 128)
            tb = n // 4
            b = self.bank()
            for kc in range(8):
                self.mm(self.ps[:, b, :], self.hn[:, kc, tok], vv[:, kc, :], kc == 0, kc == 7,
                        self.wkeys(sv) + [("hn", kc, tb)], [("ps", b)])
            vg = n % 2
            self.gelu(self.F(vg), self.ps[:, b, :], 2 + n % 2, [("ps", b)], self.fk(vg))
            for g in range(4):
                self.S.op("vector", lambda e, g=g, vg=vg: e.bn_stats(out=st[:, g, :], in_=self.F(vg)[:, g * 128:(g + 1) * 128]),
                          self.fk(vg), ["bnst"])
            for g in range(4):
                self.S.op("vector", lambda e, g=g: e.bn_aggr(out=mv[:, g, :], in_=st[:, g, :]), ["bnst"], ["bnmv"])
            self.act(sm[:, 20:24], mv[:, :, 1], AF.Sqrt, ["bnmv"], ["sm2"], bias=EPS, scale=1.0)
            self.recip(sm[:, 20:24], sm[:, 20:24], ["sm2"], ["sm2"])
            xn = 2 + n % 2
            for g in range(4):
                self.ts(self.F(xn)[:, g * 128:(g + 1) * 128], self.F(vg)[:, g * 128:(g + 1) * 128], mv[:, g, 0:1], ALU.subtract,
                        self.fk(vg) + ["bnmv", "sm2"], self.fk(xn), s2=sm[:, 20 + g:21 + g], op1=ALU.mult)
            lnv = (4, n % 2)
            self.tt(self.H(*lnv), self.F(xn), self.pr[:, PR_VG:PR_VG + 512], ALU.mult, self.fk(xn) + ["pr"], self.hk(*lnv))
            b2 = self.bank()
            for g in range(4):
                self.mm(self.ps[:, b2, g * 128:(g + 1) * 128], self.H(*lnv)[:, g * 128:(g + 1) * 128], wst[:, g, :], True, True,
                        self.hk(*lnv) + self.hk(7, 0), [("ps", b2)])
            tf = 5 + n % 2
            self.tt(self.F(tf), self.ps[:, b2, :], self.pr[:, PR_BS:PR_BS + 512], ALU.add, [("ps", b2), "pr"], self.fk(tf))
            mov = self.mo[:, 0:4, tok]
            self.tt(mov, self.F(tf).rearrange("p (g t) -> p g t", g=4), mov, ALU.mult,
                    self.fk(tf) + [("mo", c, tb) for c in range(4)], [("mo", c, tb) for c in range(4)])
        wrep = self.W(20, 2).rearrange("p (c j m) -> p c j m", c=8, j=2)
        Ra, Rb = self.F(9), self.F(10)
        self.mset(Ra[0:33, :], 0.0, self.fk(9))
        self.mset(Rb[0:33, :], 0.0, self.fk(10))
        self.mset(Ra[32:33, :], 1.0, self.fk(9))
        self.mset(Rb[0:1, :], 1.0, self.fk(10))
        for h in range(4):
            sw = self.wslot()
            vw = self.wview(sw, 8, 512)
            cols = [1024 + h * 128, 1536 + h * 128, 2048 + h * 128, 2560 + h * 128]
            self.acquire(sw)
            for j, c0 in enumerate(cols):
                self.dma(vw[:, :, j * 128:(j + 1) * 128], win[:, :, c0:c0 + 128], [], [("ws", sw, j)], eng="gpsimd")
            for j in range(2):
                self.cp(wrep[:, :, j, :], self.w8[:, :, j * 4 + h:j * 4 + h + 1].broadcast_to([128, 8, 128]), ["w8"], self.fk(20) + self.fk(21))
            self.mset(self.Sst, 0.0, ["Sst"])
            for tb in range(NTB):
                self.gdn_unit(i, h, tb, vw, wrep, sw)
        self.out_proj(self.d_w["w_out_e"][i].rearrange("(c p) n -> p c n", p=128))

    def gdn_unit(self, i, h, tb, vw, wrep, sw):
        sm = self.sm
        tsl = slice(tb * 512, (tb + 1) * 512)
        ones = self.cb[:, CB_ONES:CB_ONES + 128]
        identf = self.cf[:, CF_ID:CF_ID + 128]
        fk, hk = self.fk, self.hk
        T4 = lambda ap: ap.rearrange("p (t c) -> p t c", c=128)
        tl = lambda t: slice(t * 128, (t + 1) * 128)

        def proj(lhs_fn, wkeys):
            b = self.bank()
            for kc in range(8):
                self.mm(self.ps[:, b, :], lhs_fn(kc), self.hn[:, kc, tsl], kc == 0, kc == 7, wkeys + [("hn", kc, tb)], [("ps", b)])
            return b
        BETA, GC, G2, E1, ACC, RINV, TR, E2, DTI, DTS, DS, R, NEGB = 0, 1, 2, 3, 5, 6, 7, 8, 11, 12, 13, 14, 15
        ZS, QN, KN, KB, VC, Pa, Pb, Qa, Qb = (16, 0), (16, 1), (17, 0), (18, 0), (18, 1), (19, 0), (19, 1), (4, 0), (4, 1)
        RBF = (1, 0)
        QDEC, KBTM, VBTM, KDTM, QKT, NWT, VNEW = 0, 2, 5, 6, 3, 7, 8
        Ra, Rb = self.F(9), self.F(10)
        F, H = self.F, self.H
        b = proj(lambda kc: vw[:, kc, 384:512], [("ws", sw, 3)])
        self.act(H(*ZS), self.ps[:, b, :], AF.Silu, [("ps", b)], hk(*ZS))
        b = proj(lambda kc: wrep[:, kc, 0, :], fk(20) + fk(21))
        self.act(F(BETA), self.ps[:, b, :], AF.Sigmoid, [("ps", b)], fk(BETA))
        b = proj(lambda kc: wrep[:, kc, 1, :], fk(20) + fk(21))
        self.act(F(G2), self.ps[:, b, :], AF.Exp, [("ps", b), "pr"], fk(G2), bias=self.pr[:, PR_DT + h:PR_DT + h + 1])
        self.act(F(G2), F(G2), AF.Ln, fk(G2), fk(G2), bias=1.0)
        self.ts(F(G2), F(G2), sm[:, 8 + h:9 + h], ALU.mult, fk(G2) + ["sm"], fk(G2))
        for t in range(4):
            self.S.op("vector", lambda e, t=t: e.tensor_tensor_scan(out=F(GC)[:, tl(t)], data0=ones, data1=F(G2)[:, tl(t)],
                                                                    initial=0.0, op0=ALU.mult, op1=ALU.add),
                      fk(G2) + ["cb"], fk(GC))
        for which, dst in enumerate([QN, KN, VC]):
            b = proj(lambda kc, which=which: vw[:, kc, which * 128:(which + 1) * 128], [("ws", sw, which)])
            if tb == 0:
                self.mset(self.xp[:, 0:3], 0.0, ["xp"])
            else:
                self.cp(self.xp[:, 0:3], self.hal[:, which, 0:3], [("hal", which)], ["xp"])
            self.act(self.xp[:, 3:515], self.ps[:, b, :], AF.Copy, [("ps", b)], ["xp"])
            cc = which * 4 + h
            wc = lambda j: self.pc[:, PC_CONV + i * 48 + cc * 4 + j:PC_CONV + i * 48 + cc * 4 + j + 1]
            self.ts(F(ACC), self.xp[:, 3:515], wc(3), ALU.mult, ["xp", "pc"], fk(ACC))
            for j in (2, 1, 0):
                self.stt(F(ACC), self.xp[:, j:j + 512], wc(j), F(ACC), ALU.mult, ALU.add, ["xp", "pc"] + fk(ACC), fk(ACC))
            self.cp(self.hal[:, which, 0:3], self.xp[:, 512:515], ["xp"], [("hal", which)])
            self.act(H(*dst), F(ACC), AF.Silu, fk(ACC), hk(*dst))
        for src, scl in ((QN, 128.0 ** -0.5), (KN, 1.0)):
            self.act(H(*Pa), H(*src), AF.Square, hk(*src), hk(*Pa))
            b = self.bank()
            self.mm(self.ps[:, b, :], ones, H(*Pa), True, True, hk(*Pa) + ["cb"], [("ps", b)])
            self.act(F(RINV), self.ps[:, b, :], AF.Sqrt, [("ps", b)], fk(RINV), bias=EPS, scale=1.0)
            self.recip(F(RINV), F(RINV), fk(RINV), fk(RINV))
            self.stt(H(*src), H(*src), scl, F(RINV), ALU.mult, ALU.mult, hk(*src) + fk(RINV), hk(*src))
        self.act(F(E1), F(GC), AF.Exp, fk(GC), fk(E1))
        self.tt(H(*KB), H(*KN), F(BETA), ALU.mult, hk(*KN) + fk(BETA), hk(*KB))
        self.tt(F(TR), H(*KB), F(E1), ALU.mult, hk(*KB) + fk(E1), fk(TR))
        b = self.bank()
        for t in range(4):
            self.tr(self.ps[:, b, tl(t)], F(TR)[:, tl(t)], fk(TR), [("ps", b)])
        self.act(F(KBTM), self.ps[:, b, :], AF.Copy, [("ps", b)], fk(KBTM))
        self.tt(F(TR), H(*VC), F(BETA), ALU.mult, hk(*VC) + fk(BETA), fk(TR))
        b = self.bank()
        for t in range(4):
            self.tr(self.ps[:, b, tl(t)], F(TR)[:, tl(t)], fk(TR), [("ps", b)])
        self.act(F(VBTM), self.ps[:, b, :], AF.Copy, [("ps", b)], fk(VBTM))
        self.tt(F(QDEC), H(*QN), F(E1), ALU.mult, hk(*QN) + fk(E1), fk(QDEC))
        self.tt(T4(F(E2)), T4(F(GC))[:, :, 127:128].broadcast_to([128, 4, 128]), T4(F(GC)), ALU.subtract, fk(GC), fk(E2))
        self.act(F(E2), F(E2), AF.Exp, fk(E2), fk(E2))
        self.tt(F(TR), H(*KN), F(E2), ALU.mult, hk(*KN) + fk(E2), fk(TR))
        b = self.bank()
        for t in range(4):
            self.tr(self.ps[:, b, tl(t)], F(TR)[:, tl(t)], fk(TR), [("ps", b)])
        self.act(F(KDTM), self.ps[:, b, :], AF.Copy, [("ps", b)], fk(KDTM))
        self.act(sm[:, 16:20], T4(F(GC))[:, :, 127], AF.Exp, fk(GC), ["sm3"])
        self.cp(Ra[0:1, :], F(GC)[0:1, :], fk(GC), fk(9))
        self.ts(Rb[32:33, :], F(GC)[32:33, :], -1.0, ALU.mult, fk(GC), fk(10))
        b = self.bank()
        for t in range(4):
            self.mm(self.ps[:, b, tl(t)], Rb[0:33, tl(t)], Ra[0:33, tl(t)], True, False, fk(9) + fk(10), [("ps", b)])
            self.mm(self.ps[:, b, tl(t)], identf, self.cf[:, CF_MJI:CF_MJI + 128], False, True, ["cf"], [("ps", b)])
        self.act(F(DTI), self.ps[:, b, :], AF.Exp, [("ps", b)], fk(DTI))
        self.tt(T4(F(DTS)), T4(F(DTI)), self.cf[:, CF_SM:CF_SM + 128].rearrange("p (o t) -> p o t", o=1).broadcast_to([128, 4, 128]),
                ALU.mult, fk(DTI) + ["cf"], fk(DTS), eng="gpsimd")
        b = self.bank()
        for t in range(4):
            self.mm(self.ps[:, b, tl(t)], Ra[0:33, tl(t)], Rb[0:33, tl(t)], True, False, fk(9) + fk(10), [("ps", b)])
            self.mm(self.ps[:, b, tl(t)], identf, self.cf[:, CF_MIJ:CF_MIJ + 128], False, True, ["cf"], [("ps", b)])
        self.act(F(DS), self.ps[:, b, :], AF.Exp, [("ps", b)], fk(DS))
        b = self.bank()
        for t in range(4):
            self.mm(self.ps[:, b, tl(t)], H(*KN)[:, tl(t)], H(*QN)[:, tl(t)], True, True, hk(*KN) + hk(*QN), [("ps", b)])
        self.tt(F(QKT), self.ps[:, b, :], F(DTI), ALU.mult, [("ps", b)] + fk(DTI), fk(QKT))
        bc = lambda ap: ap.rearrange("p (o t) -> p o t", o=1).broadcast_to([128, 4, 128])
        bd32 = bc(self.cf[:, CF_BD32:CF_BD32 + 128])
        I4 = bc(identf)
        Pc, Pn, Qc, Qn, EE, ET, LU, AA = 11, 12, 13, 19, 1, 17, 18, 4
        b = self.bank()
        for t in range(4):
            self.mm(self.ps[:, b, tl(t)], H(*KN)[:, tl(t)], H(*KB)[:, tl(t)], True, True, hk(*KN) + hk(*KB), [("ps", b)])
        self.stt(F(NEGB), self.ps[:, b, :], -1.0, F(DTS), ALU.mult, ALU.mult, [("ps", b)] + fk(DTS), fk(NEGB))
        b = self.bank()
        for t in range(4):
            self.mm(self.ps[:, b, tl(t)], H(*KB)[:, tl(t)], H(*KN)[:, tl(t)], True, True, hk(*KN) + hk(*KB), [("ps", b)])
        self.tt(F(AA), self.ps[:, b, :], F(DS), ALU.mult, [("ps", b)] + fk(DS), fk(AA))
        self.stt(T4(F(Pc)), T4(F(NEGB)), -1.0, bd32, ALU.mult, ALU.mult, fk(NEGB) + ["cf"], fk(Pc))
        self.tt(T4(F(Qc)), T4(F(AA)), bd32, ALU.mult, fk(AA) + ["cf"], fk(Qc), eng="gpsimd")
        self.stt(F(LU), F(NEGB), -1.0, F(Pc), ALU.mult, ALU.subtract, fk(NEGB) + fk(Pc), fk(LU))
        self.tt(T4(F(EE)), I4, T4(F(Pc)), ALU.subtract, fk(Pc) + ["cf"], fk(EE))
        self.tt(T4(F(ET)), I4, T4(F(Qc)), ALU.subtract, fk(Qc) + ["cf"], fk(ET), eng="gpsimd")
        for k in range(4):
            b = self.bank()
            for t in range(4):
                self.mm(self.ps[:, b, tl(t)], F(Pc)[:, tl(t)], F(Qc)[:, tl(t)], True, True, fk(Pc) + fk(Qc), [("ps", b)])
            self.act(F(Qn), self.ps[:, b, :], AF.Copy, [("ps", b)], fk(Qn))
            b = self.bank()
            for t in range(4):
                self.mm(self.ps[:, b, tl(t)], F(Qc)[:, tl(t)], F(Pc)[:, tl(t)], True, True, fk(Pc) + fk(Qc), [("ps", b)])
            self.cp(F(Pn), self.ps[:, b, :], [("ps", b)], fk(Pn))
            b = self.bank()
            for t in range(4):
                self.mm(self.ps[:, b, tl(t)], F(Qn)[:, tl(t)], F(EE)[:, tl(t)], True, True, fk(Qn) + fk(EE), [("ps", b)])
            self.tt(F(EE), F(EE), self.ps[:, b, :], ALU.add, fk(EE) + [("ps", b)], fk(EE))
            b = self.bank()
            for t in range(4):
                self.mm(self.ps[:, b, tl(t)], F(Pn)[:, tl(t)], F(ET)[:, tl(t)], True, True, fk(Pn) + fk(ET), [("ps", b)])
            self.tt(F(ET), F(ET), self.ps[:, b, :], ALU.add, fk(ET) + [("ps", b)], fk(ET))
            Pc, Pn = Pn, Pc
            Qc, Qn = Qn, Qc
        MM, MT, WTT, WT = 11, 12, 13, 19
        b = self.bank()
        for t in range(4):
            self.mm(self.ps[:, b, tl(t)], F(ET)[:, tl(t)], F(LU)[:, tl(t)], True, True, fk(ET) + fk(LU), [("ps", b)])
        self.act(F(MM), self.ps[:, b, :], AF.Copy, [("ps", b)], fk(MM))
        b = self.bank()
        for t in range(4):
            self.mm(self.ps[:, b, tl(t)], F(LU)[:, tl(t)], F(ET)[:, tl(t)], True, True, fk(ET) + fk(LU), [("ps", b)])
        self.cp(F(MT), self.ps[:, b, :], [("ps", b)], fk(MT))
        b = self.bank()
        for t in range(4):
            self.mm(self.ps[:, b, tl(t)], F(MM)[:, tl(t)], F(MT)[:, tl(t)], True, True, fk(MM) + fk(MT), [("ps", b)])
        self.tt(T4(F(WTT)), T4(self.ps[:, b, :]), I4, ALU.add, [("ps", b), "cf"], fk(WTT))
        b = self.bank()
        for t in range(4):
            self.mm(self.ps[:, b, tl(t)], F(MM)[:, tl(t)], F(WTT)[:, tl(t)], True, True, fk(MM) + fk(WTT), [("ps", b)])
        self.tt(F(WT), F(WTT), self.ps[:, b, :], ALU.subtract, fk(WTT) + [("ps", b)], fk(WT))
        b = self.bank()
        for t in range(4):
            self.mm(self.ps[:, b, tl(t)], F(WT)[:, tl(t)], F(EE)[:, tl(t)], True, True, fk(WT) + fk(EE), [("ps", b)])
        self.act(F(R), self.ps[:, b, :], AF.Copy, [("ps", b)], fk(R))
        b = self.bank()
        for t in range(4):
            self.mm(self.ps[:, b, tl(t)], F(KBTM)[:, tl(t)], F(R)[:, tl(t)], True, True, fk(KBTM) + fk(R), [("ps", b)])
        self.act(F(NWT), self.ps[:, b, :], AF.Copy, [("ps", b)], fk(NWT), scale=-1.0)
        pO, pV, pS = 4, 5, 6
        for t in range(4):
            self.mm(self.ps[:, pV, 0:128], F(R)[:, tl(t)], F(VBTM)[:, tl(t)], True, False, fk(R) + fk(VBTM), [("ps", pV)])
            self.mm(self.ps[:, pV, 0:128], F(NWT)[:, tl(t)], self.Sst, False, True, fk(NWT) + ["Sst"], [("ps", pV)])
            self.act(F(VNEW)[:, 0:128], self.ps[:, pV, 0:128], AF.Copy, [("ps", pV)], fk(VNEW))
            self.mm(self.ps[:, pO, tl(t)], self.Sst, F(QDEC)[:, tl(t)], True, False, ["Sst"] + fk(QDEC), [("ps", pO)])
            self.mm(self.ps[:, pO, tl(t)], F(VNEW)[:, 0:128], F(QKT)[:, tl(t)], False, True, fk(VNEW) + fk(QKT), [("ps", pO)])
            self.mm(self.ps[:, pS, 0:128], F(KDTM)[:, tl(t)], F(VNEW)[:, 0:128], True, True, fk(KDTM) + fk(VNEW), [("ps", pS)])
            self.stt(self.Sst, self.Sst, sm[:, 16 + t:17 + t], self.ps[:, pS, 0:128], ALU.mult, ALU.add,
                     ["Sst", "sm3", ("ps", pS)], ["Sst"])
        self.act(H(*VC), self.ps[:, pO, :], AF.Square, [("ps", pO)], hk(*VC))
        b = self.bank()
        self.mm(self.ps[:, b, :], ones, H(*VC), True, True, hk(*VC) + ["cb"], [("ps", b)])
        self.act(F(DTS), self.ps[:, b, :], AF.Sqrt, [("ps", b)], fk(DTS), bias=EPS, scale=1.0 / 128)
        self.recip(F(DTS), F(DTS), fk(DTS), fk(DTS))
        self.stt(F(DS), self.ps[:, pO, :], self.pc[:, PC_OG + i:PC_OG + i + 1], F(DTS), ALU.mult, ALU.mult,
                 [("ps", pO), "pc"] + fk(DTS), fk(DS))
        self.tt(self.mo[:, 4 + h, tsl], F(DS), H(*ZS), ALU.mult, fk(DS) + hk(*ZS), [("mo", 4 + h, tb)])


_CACHE = {}


def _get_prog(layers):
    key = tuple(l % 2 for l in layers)
    if key not in _CACHE:
        _CACHE[key] = KB(layers)
    return _CACHE[key]


def _host_inputs(inp, layers):
    layers = list(layers)
    ev = [l // 2 for l in layers if l % 2 == 0]
    od = [l // 2 for l in layers if l % 2 == 1]
    cb, cf, al = make_consts()
    pc, pr = pack_params(inp, layers)
    sel = {"e": ev, "o": od, "a": layers}
    shared = {}
    for n in WNAMES:
        idx = sel[WKIND[n]]
        if idx:
            shared[n] = np.ascontiguousarray(np.asarray(inp[n], dtype=np.float32)[idx])
    if ev:
        shared["gmlp_wsT"] = np.ascontiguousarray(np.transpose(np.asarray(inp["gmlp_ws"], np.float32)[ev], (0, 1, 3, 2)))
    if od:
        shared["alibi"] = al
    shared["pcol"] = pc
    shared["prow"] = pr
    shared["cb"] = cb
    shared["cf"] = cf
    return shared


LAUNCH_GROUPS = [[0, 1, 2, 3]]


def kernel(**inp):
    x = np.asarray(inp["x"], np.float32)
    p = np.asarray(inp["p"], np.float32)
    B = x.shape[0]
    pT = np.transpose(p, (1, 0, 3, 2))
    hT = np.ascontiguousarray(np.transpose(x, (0, 2, 1)))
    for layers in LAUNCH_GROUPS:
        kb = _get_prog(layers)
        shared = _host_inputs(inp, layers)
        in_maps = []
        for c in range(B):
            m = dict(shared)
            m["xT"] = hT[c]
            m["pT"] = np.ascontiguousarray(pT[c][list(layers)])
            in_maps.append(m)
        res = run_bass_kernel_spmd(kb.nc, in_maps, core_ids=list(range(B)))
        hT = np.stack([np.asarray(r["outT"]) for r in res.results], axis=0)
    return np.ascontiguousarray(np.transpose(hT, (0, 2, 1))).astype(np.float32)
```
